# Optimizing a Trainium2 kernel written in Bass

```python
import jax, jax.numpy as jnp
from jax import lax
import numpy as np

D_MODEL = 1024
BATCH = 4
SEQ = 4096
DEPTH = 2

HEAD_DIM = 64
BLOCK = 128
ROPE_THETA = 10000.0
EPS = 1e-6
A_HEADS = 8
IDX_HEADS = 4
IDX_DIM = 64
TOPK_MAX = 256
B_Q_HEADS = 8
B_KV_HEADS = 2
WINDOW = 128
C_HEADS = 8

A_WIDTH = A_HEADS * HEAD_DIM
B_WIDTH = B_Q_HEADS * HEAD_DIM
B_KV_WIDTH = B_KV_HEADS * HEAD_DIM
C_WIDTH = C_HEADS * HEAD_DIM

SPLIT_SIZES = (
    A_WIDTH, A_WIDTH, A_WIDTH, A_WIDTH,
    IDX_HEADS * IDX_DIM, IDX_DIM, IDX_HEADS,
    B_WIDTH, B_KV_WIDTH, B_KV_WIDTH, B_WIDTH,
    C_WIDTH, C_WIDTH, C_WIDTH, C_WIDTH,
    D_MODEL, D_MODEL, D_MODEL,
)
D_IN = int(sum(SPLIT_SIZES))
SPLIT_OFFSETS = [int(o) for o in np.cumsum(SPLIT_SIZES)[:-1]]

kernel_name = "hybrid_dsa_swa_stickbreak_gated_trunk"


def rms_norm(x, g):
    xf = x.astype(jnp.float32)
    y = xf * lax.rsqrt(jnp.mean(xf * xf, axis=-1, keepdims=True) + EPS)
    return (y * g.astype(jnp.float32)).astype(x.dtype)


def rope_tables(seq, dim):
    inv = 1.0 / (ROPE_THETA ** (jnp.arange(0, dim, 2, dtype=jnp.float32) / dim))
    ang = jnp.arange(seq, dtype=jnp.float32)[:, None] * inv[None, :]
    return jnp.cos(ang), jnp.sin(ang)


def apply_rope(x, cos, sin):
    x1, x2 = jnp.split(x.astype(jnp.float32), 2, axis=-1)
    c = cos[None, :, None, :]
    s = sin[None, :, None, :]
    return jnp.concatenate([x1 * c - x2 * s, x1 * s + x2 * c], axis=-1).astype(x.dtype)


def dsa_attention(q, k, v, iq, ik, iw):
    bsz, seq, n_heads, dh = q.shape
    topk = min(TOPK_MAX, seq // 4)
    n_blocks = seq // BLOCK
    key_pos = jnp.arange(seq)
    w_scale = (IDX_HEADS ** -0.5) * (IDX_DIM ** -0.5)
    ikf = ik.astype(jnp.float32)

    def block(i):
        t0 = i * BLOCK
        qb = lax.dynamic_slice_in_dim(q, t0, BLOCK, axis=1)
        iqb = lax.dynamic_slice_in_dim(iq, t0, BLOCK, axis=1).astype(jnp.float32)
        iwb = lax.dynamic_slice_in_dim(iw, t0, BLOCK, axis=1).astype(jnp.float32)
        qpos = t0 + jnp.arange(BLOCK)
        causal = key_pos[None, :] <= qpos[:, None]
        dots = jnp.einsum('bthd,bsd->bths', iqb, ikf)
        score = jnp.einsum('bth,bths->bts', iwb * w_scale, jax.nn.relu(dots))
        score = jnp.where(causal[None], score, -jnp.inf)
        top_val, top_idx = lax.top_k(score, topk)
        valid = jnp.isfinite(top_val)
        k_sel = jax.vmap(lambda kk, ii: kk[ii])(k, top_idx)
        v_sel = jax.vmap(lambda vv, ii: vv[ii])(v, top_idx)
        logits = jnp.einsum('bthd,btkhd->bhtk', qb, k_sel).astype(jnp.float32) * (dh ** -0.5)
        logits = jnp.where(valid[:, None], logits, -jnp.inf)
        p = jax.nn.softmax(logits, axis=-1)
        return jnp.einsum('bhtk,btkhd->bthd', p.astype(v.dtype), v_sel)

    out = lax.map(block, jnp.arange(n_blocks))
    return out.transpose(1, 0, 2, 3, 4).reshape(bsz, seq, n_heads, dh)


def swa_sinks_attention(q, k, v, sinks):
    bsz, seq, hq, dh = q.shape
    hkv = k.shape[2]
    grp = hq // hkv
    nb = seq // BLOCK
    qb = q.reshape(bsz, nb, BLOCK, hkv, grp, dh)

    def band(t):
        tb = t.reshape(bsz, nb, BLOCK, hkv, dh)
        prev = jnp.concatenate([jnp.zeros_like(tb[:, :1]), tb[:, :-1]], axis=1)
        return jnp.concatenate([prev, tb], axis=2)

    kb, vb = band(k), band(v)
    logits = jnp.einsum('bnqhgd,bnkhd->bnhgqk', qb, kb).astype(jnp.float32) * (dh ** -0.5)
    blk = jnp.arange(nb)[:, None, None]
    qpos = blk * BLOCK + jnp.arange(BLOCK)[None, :, None]
    kpos = blk * BLOCK - BLOCK + jnp.arange(2 * BLOCK)[None, None, :]
    allowed = (kpos <= qpos) & (kpos > qpos - WINDOW) & (kpos >= 0)
    logits = jnp.where(allowed[None, :, None, None], logits, -jnp.inf)
    sink = jnp.broadcast_to(
        sinks.astype(jnp.float32).reshape(hkv, grp)[None, None, :, :, None, None],
        logits.shape[:-1] + (1,))
    p = jax.nn.softmax(jnp.concatenate([logits, sink], axis=-1), axis=-1)[..., :-1]
    out = jnp.einsum('bnhgqk,bnkhd->bnqhgd', p.astype(v.dtype), vb)
    return out.reshape(bsz, seq, hq, dh)


def stick_breaking_attention(q, k, v):
    bsz, seq, n_heads, dh = q.shape
    nb = seq // BLOCK
    key_pos = jnp.arange(seq)

    def block(i):
        t0 = i * BLOCK
        qb = lax.dynamic_slice_in_dim(q, t0, BLOCK, axis=1)
        qpos = t0 + jnp.arange(BLOCK)
        strict = (key_pos[None, :] < qpos[:, None])[None, None]
        z = jnp.einsum('bthd,bshd->bhts', qb, k).astype(jnp.float32) * (dh ** -0.5)
        log_beta = jax.nn.log_sigmoid(z)
        log_1m = jnp.where(strict, jax.nn.log_sigmoid(-z), 0.0)
        after = lax.cumsum(log_1m, axis=3, reverse=True) - log_1m
        a = jnp.where(strict, jnp.exp(log_beta + after), 0.0)
        return jnp.einsum('bhts,bshd->bthd', a.astype(v.dtype), v)

    out = lax.map(block, jnp.arange(nb))
    return out.transpose(1, 0, 2, 3, 4).reshape(bsz, seq, n_heads, dh)


def hybrid_layer(x, c_silu, cos, sin, norm_g, w_ada, b_ada, w_in, sinks, w_br_a, w_br_b, w_br_c, w_out):
    bsz, seq, _ = x.shape
    mod = c_silu @ w_ada + b_ada
    shift, scale, gate = jnp.split(mod, 3, axis=-1)
    u = rms_norm(x, norm_g) * (1 + scale[:, None]) + shift[:, None]
    z = u @ w_in
    (aq, ak, av, ag, iq, ik, iw, bq, bk, bv, bg, cq, ck, cv, cg, ma, mb, mc) = jnp.split(z, SPLIT_OFFSETS, axis=-1)

    def heads(t, n):
        return t.reshape(bsz, seq, n, -1)

    ya = dsa_attention(
        apply_rope(heads(aq, A_HEADS), cos, sin),
        apply_rope(heads(ak, A_HEADS), cos, sin),
        heads(av, A_HEADS),
        apply_rope(heads(iq, IDX_HEADS), cos, sin),
        apply_rope(ik[:, :, None, :], cos, sin)[:, :, 0, :],
        iw)
    ya = ya.reshape(bsz, seq, A_WIDTH) * jax.nn.silu(ag)
    yb = swa_sinks_attention(
        apply_rope(heads(bq, B_Q_HEADS), cos, sin),
        apply_rope(heads(bk, B_KV_HEADS), cos, sin),
        heads(bv, B_KV_HEADS),
        sinks)
    yb = yb.reshape(bsz, seq, B_WIDTH) * jax.nn.silu(bg)
    yc = stick_breaking_attention(heads(cq, C_HEADS), heads(ck, C_HEADS), heads(cv, C_HEADS))
    yc = yc.reshape(bsz, seq, C_WIDTH) * jax.nn.silu(cg)

    merged = (jax.nn.sigmoid(ma) * (ya @ w_br_a)
              + jax.nn.sigmoid(mb) * (yb @ w_br_b)
              + jax.nn.sigmoid(mc) * (yc @ w_br_c))
    return x + gate[:, None] * (merged @ w_out)


def setup_inputs(seed: int = 0) -> dict:
    key = jax.random.key(seed)
    ks = jax.random.split(key, 13)
    f32 = jnp.float32

    def nrm(k, shape, scale):
        return jax.random.normal(k, shape, dtype=f32) * scale

    return {
        "x": nrm(ks[0], (BATCH, SEQ, D_MODEL), 1.0),
        "c": nrm(ks[1], (BATCH, D_MODEL), 1.0),
        "norm_g": 1.0 + nrm(ks[2], (DEPTH, D_MODEL), 0.02),
        "w_ada": nrm(ks[3], (DEPTH, D_MODEL, 3 * D_MODEL), 0.2 * D_MODEL ** -0.5),
        "b_ada": nrm(ks[4], (DEPTH, 3 * D_MODEL), 0.02),
        "w_in": nrm(ks[5], (DEPTH, D_MODEL, D_IN), D_MODEL ** -0.5),
        "sinks": nrm(ks[6], (DEPTH, B_Q_HEADS), 0.5),
        "w_br_a": nrm(ks[7], (DEPTH, A_WIDTH, D_MODEL), A_WIDTH ** -0.5),
        "w_br_b": nrm(ks[8], (DEPTH, B_WIDTH, D_MODEL), B_WIDTH ** -0.5),
        "w_br_c": nrm(ks[9], (DEPTH, C_WIDTH, D_MODEL), C_WIDTH ** -0.5),
        "w_out": nrm(ks[10], (DEPTH, D_MODEL, D_MODEL), D_MODEL ** -0.5),
        "final_g": 1.0 + nrm(ks[11], (D_MODEL,), 0.02),
    }


def reference(x, c, norm_g, w_ada, b_ada, w_in, sinks, w_br_a, w_br_b, w_br_c, w_out, final_g):
    seq = x.shape[1]
    cos, sin = rope_tables(seq, HEAD_DIM)
    c_silu = jax.nn.silu(c)
    h = x
    for l in range(DEPTH):
        h = hybrid_layer(h, c_silu, cos, sin, norm_g[l], w_ada[l], b_ada[l], w_in[l], sinks[l],
                         w_br_a[l], w_br_b[l], w_br_c[l], w_out[l])
    return rms_norm(h, final_g)
```

```python
import numpy as np
import ml_dtypes
from contextlib import ExitStack
import concourse.bass as bass
import concourse.mybir as mybir
from concourse.bass_utils import run_bass_kernel_spmd

F32 = mybir.dt.float32
BF16 = mybir.dt.bfloat16
ALU = mybir.AluOpType
AF = mybir.ActivationFunctionType
AX = mybir.AxisListType

S = 4096
D = 1024
NOWN = 2048
EPS = 1e-6
D_IN = 8772
OFF = dict(aq=0, ak=512, av=1024, ag=1536, iq=2048, ik=2304, iw=2368, bq=2372, bk=2884, bv=3012,
           bg=3140, cq=3652, ck=4164, cv=4676, cg=5188, ma=5700, mb=6724, mc=7748)
ROT_FAMS = [("aq", 512), ("ak", 512), ("iq", 256), ("ik", 64), ("bq", 512), ("bk", 128)]
ROT = {}
_o = 0
for _n, _w in ROT_FAMS:
    ROT[_n] = _o
    _o += _w
N_ROT = _o
NEG = -1.0e30
NBIS = 26


class Counter:
    def __init__(self, sem, name):
        self.sem = sem
        self.name = name
        self.value = 0


class Buf:
    __slots__ = ("name", "w", "r")

    def __init__(self, name=""):
        self.name = name
        self.w = None
        self.r = {}


class EngW:
    def __init__(self, eng, name, sem, same_sync=True):
        self.eng = eng
        self.name = name
        self.ctr = Counter(sem, name)
        self.waited = {}
        self.same_sync = same_sync


class DmaQ:
    def __init__(self, engw, sems):
        self.engw = engw
        self.ctrs = [Counter(s, f"dma_{engw.name}_{i}") for i, s in enumerate(sems)]
        self.i = 0


class KS:
    def __init__(self, nc, stack, n_dma_sems=8):
        self.nc = nc
        mk = lambda nm: stack.enter_context(nc.semaphore(nm))
        self.pe = EngW(nc.tensor, "pe", mk("s_pe"), same_sync=False)
        self.act = EngW(nc.scalar, "act", mk("s_act"))
        self.dve = EngW(nc.vector, "dve", mk("s_dve"))
        self.pool = EngW(nc.gpsimd, "pool", mk("s_pool"))
        self.sp = EngW(nc.sync, "sp", mk("s_sp"))
        self.engs = [self.pe, self.act, self.dve, self.pool, self.sp]
        self.q_sp = DmaQ(self.sp, [mk(f"d_sp{i}") for i in range(n_dma_sems)])
        self.q_act = DmaQ(self.act, [mk(f"d_act{i}") for i in range(n_dma_sems)])
        self.q_pool = DmaQ(self.pool, [mk(f"d_pool{i}") for i in range(n_dma_sems)])
        self.qs = [self.q_sp, self.q_act, self.q_pool]
        self._rr = 0

    def _deps(self, reads, writes):
        deps = {}
        for b in reads:
            if b.w is not None:
                c, v = b.w
                if deps.get(c, 0) < v:
                    deps[c] = v
        for b in writes:
            if b.w is not None:
                c, v = b.w
                if deps.get(c, 0) < v:
                    deps[c] = v
            for c, v in b.r.items():
                if deps.get(c, 0) < v:
                    deps[c] = v
        return deps

    def _wait(self, E, deps, is_dma=False):
        for c, v in deps.items():
            if c is E.ctr and not is_dma and not E.same_sync:
                continue
            if E.waited.get(c, 0) >= v:
                continue
            E.eng.wait_ge(c.sem, v)
            E.waited[c] = v

    def op(self, E, fn, reads=(), writes=()):
        self._wait(E, self._deps(reads, writes))
        inst = fn(E.eng)
        E.ctr.value += 1
        inst.then_inc(E.ctr.sem, 1)
        v = E.ctr.value
        for b in reads:
            b.r[E.ctr] = v
        for b in writes:
            b.w = (E.ctr, v)
            b.r = {}
        return inst

    def dma(self, Q, out, in_, reads=(), writes=(), **kw):
        if Q is None:
            Q = self.qs[self._rr % 2]
            self._rr += 1
        E = Q.engw
        c = Q.ctrs[Q.i % len(Q.ctrs)]
        Q.i += 1
        deps = self._deps(reads, writes)
        if c.value > 0 and deps.get(c, 0) < c.value:
            deps[c] = c.value
        self._wait(E, deps, is_dma=True)
        inst = E.eng.dma_start(out=out, in_=in_, **kw)
        c.value += 16
        inst.then_inc(c.sem, 16)
        for b in reads:
            b.r[c] = c.value
        for b in writes:
            b.w = (c, c.value)
            b.r = {}
        return inst

    def barrier(self):
        targets = [(e.ctr, e.ctr.value) for e in self.engs if e.ctr.value > 0]
        for q in self.qs:
            for c in q.ctrs:
                if c.value > 0:
                    targets.append((c, c.value))
        for E in self.engs:
            for c, v in targets:
                if c is E.ctr:
                    continue
                if E.waited.get(c, 0) >= v:
                    continue
                E.eng.wait_ge(c.sem, v)
                E.waited[c] = v


class Ring:
    def __init__(self, tiles):
        self.tiles = tiles
        self.bufs = [Buf() for _ in tiles]
        self.i = 0

    def next(self):
        k = self.i % len(self.tiles)
        self.i += 1
        return self.tiles[k], self.bufs[k]


def build_program(final_norm, dbg=(), stop_after=None, mixers="ABC"):
    nc = bass.Bass("TRN2", target_bir_lowering=False)

    def din(name, shape, dt=F32):
        return nc.dram_tensor(name, list(shape), dt, kind="ExternalInput").ap()

    def dscr(name, shape, dt=BF16):
        kind = "ExternalOutput" if name in dbg else "Internal"
        return nc.dram_tensor(name, list(shape), dt, kind=kind).ap()

    x_all = din("x_all", [S, D])
    x_own = din("x_own", [NOWN, D])
    c_col = din("c_col", [128, 8])
    w_ada = din("w_ada", [D, 3 * D])
    b_ada = din("b_ada", [1, 3 * D])
    norm_g = din("norm_g", [1, D])
    w_in = din("w_in", [D, D_IN])
    w_rot = din("w_rot", [D, N_ROT])
    sinks = din("sinks", [1, 8])
    w_br = din("w_br", [3, 512, D])
    w_out = din("w_out", [D, D])
    final_g = din("final_g", [1, D])
    cos_all = din("cos_all", [128, S])
    sin_all = din("sin_all", [128, S])
    cos_own = din("cos_own", [128, NOWN])
    sin_own = din("sin_own", [128, NOWN])
    cmask_d = din("cmask", [128, 8 * 512], BF16)
    qmask_d = din("qmask", [128, 4 * 1024], BF16)
    bmask_d = din("bmask", [128, 3 * 128], BF16)
    ident_d = din("ident", [128, 128], BF16)
    triu_d = din("triu", [128, 128])
    bisw_d = din("bisw", [128, NBIS])
    y_out = nc.dram_tensor("y_own", [NOWN, D], F32, kind="ExternalOutput").ap()

    A_qT = dscr("A_qT", [512, NOWN]); A_kT = dscr("A_kT", [512, S]); A_gT = dscr("A_gT", [512, NOWN])
    A_va = dscr("A_va", [S, 8 * 65])
    I_qT = dscr("I_qT", [256, NOWN]); I_kT = dscr("I_kT", [64, S]); I_w = dscr("I_w", [NOWN, 4], F32)
    B_qT = dscr("B_qT", [512, NOWN]); B_kT = dscr("B_kT", [128, S]); B_gT = dscr("B_gT", [512, NOWN])
    B_va = dscr("B_va", [S, 2 * 65])
    C_qT = dscr("C_qT", [512, NOWN]); C_kT = dscr("C_kT", [512, S]); C_gT = dscr("C_gT", [512, NOWN])
    C_v = dscr("C_v", [S, 512])
    M_T = dscr("M_T", [3 * D, NOWN])
    MaskT = dscr("MaskT", [S, NOWN])
    Y_T = dscr("Y_T", [3 * 512, NOWN])

    with ExitStack() as st:
        K = KS(nc, st)
        _uid = [0]

        def sbuf(name, shape, dt, stack=st):
            _uid[0] += 1
            return stack.enter_context(nc.sbuf_tensor(f"sb{_uid[0]}_{name}", list(shape), dt))
        psum = [st.enter_context(nc.psum_tensor(f"ps{i}", [128, 512], F32)) for i in range(8)]
        PS = [Buf(f"ps{i}") for i in range(8)]

        ident = sbuf("ident", [128, 128], BF16); IDENT = Buf()
        triu = sbuf("triu", [128, 128], F32); TRIU = Buf()
        onesM = sbuf("onesM", [128, 128], F32); ONES = Buf()
        A_bc = sbuf("A_bc", [128, D], F32); ABC = Buf()
        sh_bc = sbuf("sh_bc", [128, D], F32); SHBC = Buf()
        gt_bc = sbuf("gt_bc", [128, D], F32); GTBC = Buf()
        esink = sbuf("esink", [128, 8], F32); ESINK = Buf()
        K.dma(K.q_sp, ident[:], ident_d[:, :], writes=[IDENT])
        K.dma(K.q_act, triu[:], triu_d[:, :], writes=[TRIU])
        K.op(K.pool, lambda e: e.memset(onesM[:], 1.0), [], [ONES])
        K.dma(K.q_sp, esink[:], sinks.partition_broadcast(128), writes=[ESINK])
        K.op(K.act, lambda e: e.activation(out=esink[:], in_=esink[:], func=AF.Exp), [ESINK], [ESINK])

        with ExitStack() as ph:
            c_sb = sbuf("c_sb", [128, 8], F32, ph); CSB = Buf()
            sc = sbuf("sc", [128, 8], F32, ph); SC = Buf()
            modrow = sbuf("modrow", [1, 3 * D], F32, ph); MOD = Buf()
            brow = sbuf("brow", [1, 3 * D], F32, ph); BROW = Buf()
            grow = sbuf("grow", [1, D], F32, ph); GROW = Buf()
            arow = sbuf("arow", [1, D], F32, ph); AROW = Buf()
            wa = Ring([sbuf(f"wa{i}", [128, 8, 512], F32, ph) for i in range(2)])
            K.dma(K.q_sp, c_sb[:], c_col[:, :], writes=[CSB])
            K.dma(K.q_act, brow[:], b_ada[:, :], writes=[BROW])
            K.dma(K.q_act, grow[:], norm_g[:, :], writes=[GROW])
            K.op(K.act, lambda e: e.activation(out=sc[:], in_=c_sb[:], func=AF.Silu), [CSB], [SC])
            for cg in range(6):
                wt, WB = wa.next()
                K.dma(K.q_sp, wt[:], w_ada[:, cg * 512:(cg + 1) * 512].rearrange("(k p) n -> p k n", p=128), writes=[WB])
                pt, PB = psum[cg % 2], PS[cg % 2]
                for k in range(8):
                    K.op(K.pe, lambda e, k=k: e.matmul(pt[0:1, :], lhsT=sc[:, k:k + 1], rhs=wt[:, k, :],
                                                       start=(k == 0), stop=(k == 7)), [SC, WB], [PB])
                K.op(K.dve, lambda e: e.tensor_tensor(out=modrow[0:1, cg * 512:(cg + 1) * 512], in0=pt[0:1, :],
                                                      in1=brow[0:1, cg * 512:(cg + 1) * 512], op=ALU.add), [PB, BROW], [MOD])
            K.op(K.dve, lambda e: e.scalar_tensor_tensor(out=arow[:], in0=modrow[0:1, D:2 * D], scalar=1.0, in1=grow[:],
                                                         op0=ALU.add, op1=ALU.mult), [MOD, GROW], [AROW])
            for (row, RB, dst, DB) in ((arow[0:1, :], AROW, A_bc, ABC), (modrow[0:1, 0:D], MOD, sh_bc, SHBC),
                                       (modrow[0:1, 2 * D:3 * D], MOD, gt_bc, GTBC)):
                for h in range(2):
                    pt, PB = psum[2 + h], PS[2 + h]
                    K.op(K.pe, lambda e: e.matmul(pt[:, :], lhsT=onesM[0:1, :], rhs=row[:, h * 512:(h + 1) * 512],
                                                  start=True, stop=True), [ONES, RB], [PB])
                    K.op(K.act, lambda e: e.copy(out=dst[:, h * 512:(h + 1) * 512], in_=pt[:, :]), [PB], [DB])
        K.barrier()

        if final_norm:
            fg_bc = sbuf("fg_bc", [128, D], F32); FGBC = Buf()
            K.dma(K.q_sp, fg_bc[:], final_g.partition_broadcast(128), writes=[FGBC])

        with ExitStack() as ph:
            uT_all = sbuf("uT_all", [128, 8, S], BF16, ph)
            uT_own = sbuf("uT_own", [128, 8, NOWN], BF16, ph)
            UT = {}
            with ExitStack() as pb:
                xr = Ring([sbuf(f"xt{i}", [128, D], F32, pb) for i in range(3)])
                jr = Ring([sbuf(f"jk{i}", [128, D], F32, pb) for i in range(2)])
                tr = Ring([sbuf(f"tt{i}", [128, D], F32, pb) for i in range(2)])
                ur = Ring([sbuf(f"ub{i}", [128, D], BF16, pb) for i in range(2)])
                ssr = Ring([sbuf(f"ss{i}", [128, 4], F32, pb) for i in range(4)])
                pbank = 0
                for which, src, nblk, uT in (("all", x_all, 32, uT_all), ("own", x_own, 16, uT_own)):
                    for blk in range(nblk):
                        xt, XB = xr.next()
                        K.dma(K.q_sp, xt[:], src[blk * 128:(blk + 1) * 128, :], writes=[XB])
                        jk, JB = jr.next()
                        ss, SSB = ssr.next()
                        K.op(K.act, lambda e: e.activation(out=jk[:], in_=xt[:], func=AF.Square, accum_out=ss[:, 0:1]),
                             [XB], [JB, SSB])
                        K.op(K.act, lambda e: e.activation(out=ss[:, 1:2], in_=ss[:, 0:1], func=AF.Sqrt,
                                                           scale=1.0 / D, bias=EPS), [SSB], [SSB])
                        K.op(K.dve, lambda e: e.reciprocal(out=ss[:, 2:3], in_=ss[:, 1:2]), [SSB], [SSB])
                        tt, TB = tr.next()
                        K.op(K.dve, lambda e: e.scalar_tensor_tensor(out=tt[:], in0=xt[:], scalar=ss[:, 2:3], in1=A_bc[:],
                                                                     op0=ALU.mult, op1=ALU.mult), [XB, SSB, ABC], [TB])
                        ub, UB = ur.next()
                        K.op(K.pool, lambda e: e.tensor_tensor(out=ub[:], in0=tt[:], in1=sh_bc[:], op=ALU.add),
                             [TB, SHBC], [UB])
                        pt, PB = psum[pbank % 2], PS[pbank % 2]
                        pbank += 1
                        ptb = pt[:].bitcast(BF16)
                        for k in range(8):
                            K.op(K.pe, lambda e, k=k: e.transpose(ptb[:, k * 128:(k + 1) * 128], ub[:, k * 128:(k + 1) * 128],
                                                                  ident[:]), [UB, IDENT], [PB])
                        key = (which, blk // 4)
                        if key not in UT:
                            UT[key] = Buf()
                        K.op(K.act, lambda e: e.copy(out=uT[:, :, blk * 128:(blk + 1) * 128],
                                                     in_=ptb.rearrange("p (k t) -> p k t", k=8)), [PB], [UT[key]])
            K.barrier()
            wf = Ring([sbuf(f"wf{i}", [128, 8, 256], F32, ph) for i in range(2)])
            wb = Ring([sbuf(f"wb{i}", [128, 8, 256], BF16, ph) for i in range(2)])
            wf2 = Ring([sbuf(f"wg{i}", [128, 8, 256], F32, ph) for i in range(2)])
            wb2 = Ring([sbuf(f"wc{i}", [128, 8, 256], BF16, ph) for i in range(2)])
            cst = Ring([sbuf(f"cst{i}", [128, 512], F32, ph) for i in range(2)])
            snt = Ring([sbuf(f"snt{i}", [128, 512], F32, ph) for i in range(2)])
            t1r = Ring([sbuf(f"t1{i}", [128, 512], F32, ph) for i in range(2)])
            t2r = Ring([sbuf(f"t2{i}", [128, 512], F32, ph) for i in range(2)])
            stg = Ring([sbuf(f"stg{i}", [128, 512], BF16, ph) for i in range(3)])
            va_stg = Ring([sbuf(f"vas{i}", [128, 8, 65], BF16, ph) for i in range(2)])
            iw_stg = Ring([sbuf(f"iws{i}", [128, 4], F32, ph) for i in range(2)])
            for t_, B_ in zip(va_stg.tiles, va_stg.bufs):
                K.op(K.pool, lambda e, t_=t_: e.memset(t_[:], 1.0), [], [B_])
            psi = [0]

            tasks = []

            def load_w(src, c0, ncols, ringf, ringb):
                wt, WF = ringf.next()
                K.dma(K.q_sp, wt[:, :, 0:ncols], src[:, c0:c0 + ncols].rearrange("(k p) n -> p k n", p=128), writes=[WF])
                wbt, WBb = ringb.next()
                K.op(K.pool, lambda e: e.tensor_copy(out=wbt[:, :, 0:ncols], in_=wt[:, :, 0:ncols]), [WF], [WBb])
                return wbt, WBb

            def t_job(fam, ncols, which, kind, dest):
                uT = uT_all if which == "all" else uT_own
                ntile = 8 if which == "all" else 4
                cosd, sind = (cos_all, sin_all) if which == "all" else (cos_own, sin_own)
                c_off = OFF[fam]
                for c0 in range(0, ncols, 256):
                    ncw = min(256, ncols - c0)

                    def load(c0=c0, ncw=ncw):
                        w1 = load_w(w_in, c_off + c0, ncw, wf, wb)
                        w2 = load_w(w_rot, ROT[fam] + c0, ncw, wf2, wb2) if kind == "rope" else None
                        return (w1, w2)

                    def compute(ws, c0=c0, ncw=ncw):
                        (wbt, WBb), w2 = ws
                        if kind == "rope":
                            wrt, WRb = w2
                        for cc in range(0, ncw, 128):
                            ncc = min(128, ncw - cc)
                            for tt in range(ntile):
                                tok = slice(tt * 512, (tt + 1) * 512)
                                UB = UT[(which, tt)]
                                pt, PB = psum[psi[0] % 4], PS[psi[0] % 4]
                                psi[0] += 1
                                for k in range(8):
                                    K.op(K.pe, lambda e, k=k: e.matmul(pt[0:ncc, :], lhsT=wbt[:, k, cc:cc + ncc], rhs=uT[:, k, tok],
                                                                       start=(k == 0), stop=(k == 7)), [WBb, UB], [PB])
                                sg, SG = stg.next()
                                if kind == "rope":
                                    pt2, PB2 = psum[psi[0] % 4], PS[psi[0] % 4]
                                    psi[0] += 1
                                    for k in range(8):
                                        K.op(K.pe, lambda e, k=k: e.matmul(pt2[0:ncc, :], lhsT=wrt[:, k, cc:cc + ncc], rhs=uT[:, k, tok],
                                                                           start=(k == 0), stop=(k == 7)), [WRb, UB], [PB2])
                                    ct, CB = cst.next()
                                    stt, SB_ = snt.next()
                                    K.dma(K.q_act, ct[:], cosd[:, tok], writes=[CB])
                                    K.dma(K.q_act, stt[:], sind[:, tok], writes=[SB_])
                                    t1, T1 = t1r.next()
                                    t2, T2 = t2r.next()
                                    K.op(K.dve, lambda e: e.tensor_tensor(out=t1[0:ncc, :], in0=pt[0:ncc, :], in1=ct[0:ncc, :], op=ALU.mult),
                                         [PB, CB], [T1])
                                    K.op(K.dve, lambda e: e.tensor_tensor(out=t2[0:ncc, :], in0=pt2[0:ncc, :], in1=stt[0:ncc, :], op=ALU.mult),
                                         [PB2, SB_], [T2])
                                    K.op(K.pool, lambda e: e.tensor_tensor(out=sg[0:ncc, :], in0=t1[0:ncc, :], in1=t2[0:ncc, :], op=ALU.add),
                                         [T1, T2], [SG])
                                else:
                                    func = {"copy": AF.Copy, "silu": AF.Silu, "sigmoid": AF.Sigmoid}[kind]
                                    K.op(K.act, lambda e: e.activation(out=sg[0:ncc, :], in_=pt[0:ncc, :], func=func), [PB], [SG])
                                K.dma(K.q_sp, dest[c0 + cc:c0 + cc + ncc, tok], sg[0:ncc, :], reads=[SG])
                    tasks.append((load, compute))

            def v_job(fam, ncols, which, dest, aug_heads=0, out_f32=False):
                uT = uT_all if which == "all" else uT_own
                nblk = 32 if which == "all" else 16
                c_off = OFF[fam]
                for c0 in range(0, ncols, 256):
                    ncw = min(256, ncols - c0)

                    def load(c0=c0, ncw=ncw):
                        return load_w(w_in, c_off + c0, ncw, wf, wb)

                    def compute(ws, c0=c0, ncw=ncw):
                        wbt, WBb = ws
                        for blk in range(nblk):
                            UB = UT[(which, blk // 4)]
                            pt, PB = psum[psi[0] % 4], PS[psi[0] % 4]
                            psi[0] += 1
                            for k in range(8):
                                K.op(K.pe, lambda e, k=k: e.matmul(pt[:, 0:ncw], lhsT=uT[:, k, blk * 128:(blk + 1) * 128], rhs=wbt[:, k, 0:ncw],
                                                                   start=(k == 0), stop=(k == 7)), [WBb, UB], [PB])
                            rows = slice(blk * 128, (blk + 1) * 128)
                            if out_f32:
                                sg, SG = iw_stg.next()
                                K.op(K.act, lambda e: e.copy(out=sg[:, 0:ncw], in_=pt[:, 0:ncw]), [PB], [SG])
                                K.dma(K.q_sp, dest[rows, c0:c0 + ncw], sg[:, 0:ncw], reads=[SG])
                            elif aug_heads:
                                nh = ncw // 64
                                h0 = c0 // 64
                                sg, SG = va_stg.next()
                                K.op(K.act, lambda e: e.copy(out=sg[:, 0:nh, 0:64], in_=pt[:, 0:ncw].rearrange("p (h d) -> p h d", d=64)),
                                     [PB], [SG])
                                K.dma(K.q_sp, dest[rows, h0 * 65:(h0 + nh) * 65], sg[:, 0:nh, :].rearrange("p h d -> p (h d)"), reads=[SG])
                            else:
                                sg, SG = stg.next()
                                K.op(K.act, lambda e: e.copy(out=sg[:, 0:ncw], in_=pt[:, 0:ncw]), [PB], [SG])
                                K.dma(K.q_sp, dest[rows, c0:c0 + ncw], sg[:, 0:ncw], reads=[SG])
                    tasks.append((load, compute))

            t_job("aq", 512, "own", "rope", A_qT)
            t_job("ak", 512, "all", "rope", A_kT)
            t_job("iq", 256, "own", "rope", I_qT)
            t_job("ik", 64, "all", "rope", I_kT)
            t_job("bq", 512, "own", "rope", B_qT)
            t_job("bk", 128, "all", "rope", B_kT)
            t_job("cq", 512, "own", "copy", C_qT)
            t_job("ck", 512, "all", "copy", C_kT)
            v_job("av", 512, "all", A_va, aug_heads=8)
            v_job("bv", 128, "all", B_va, aug_heads=2)
            v_job("cv", 512, "all", C_v)
            v_job("iw", 4, "own", I_w, out_f32=True)
            t_job("ag", 512, "own", "silu", A_gT)
            t_job("bg", 512, "own", "silu", B_gT)
            t_job("cg", 512, "own", "silu", C_gT)
            t_job("ma", 3 * D, "own", "sigmoid", M_T)
            if stop_after != "B":
                nxt = tasks[0][0]()
                for ti in range(len(tasks)):
                    cur = nxt
                    if ti + 1 < len(tasks):
                        nxt = tasks[ti + 1][0]()
                    tasks[ti][1](cur)
        K.barrier()

        if stop_after == "C":
            return nc
        scale = 0.125

        def finalize(yps_t, YPB, h, j, gsrc, yrow0, extra, rings):
            dnr, rcr, bcr, t3r, gtr, sgr, bcps, BCPS = rings
            tok = slice(j * 512, (j + 1) * 512)
            dn, DN = dnr.next()
            if extra is not None:
                K.op(K.dve, lambda e: e.tensor_scalar(out=dn[64:65, :], in0=yps_t[64:65, :], scalar1=extra, scalar2=None,
                                                      op0=ALU.add), [YPB, ESINK], [DN])
                K.op(K.dve, lambda e: e.reciprocal(out=dn[64:65, :], in_=dn[64:65, :]), [DN], [DN])
            else:
                K.op(K.dve, lambda e: e.reciprocal(out=dn[64:65, :], in_=yps_t[64:65, :]), [YPB], [DN])
            K.op(K.pe, lambda e: e.matmul(bcps[0:64, :], lhsT=onesM[64:65, 0:64], rhs=dn[64:65, :], start=True, stop=True),
                 [ONES, DN], [BCPS])
            bc, BC = bcr.next()
            K.op(K.act, lambda e: e.copy(out=bc[0:64, :], in_=bcps[0:64, :]), [BCPS], [BC])
            t3, T3 = t3r.next()
            K.op(K.dve, lambda e: e.tensor_tensor(out=t3[0:64, :], in0=yps_t[0:64, :], in1=bc[0:64, :], op=ALU.mult), [YPB, BC], [T3])
            gt, GT = gtr.next()
            K.dma(K.q_act, gt[0:64, :], gsrc[h * 64:(h + 1) * 64, tok], writes=[GT])
            sg, SG = sgr.next()
            K.op(K.pool, lambda e: e.tensor_tensor(out=sg[0:64, :], in0=t3[0:64, :], in1=gt[0:64, :], op=ALU.mult), [T3, GT], [SG])
            K.dma(K.q_sp, Y_T[yrow0 + h * 64:yrow0 + (h + 1) * 64, tok], sg[0:64, :], reads=[SG])

        def fin_rings(ph):
            return (Ring([sbuf(f"dn{i}", [128, 512], F32, ph) for i in range(2)]), None,
                    Ring([sbuf(f"bc{i}", [64, 512], F32, ph) for i in range(2)]),
                    Ring([sbuf(f"t3{i}", [64, 512], F32, ph) for i in range(2)]),
                    Ring([sbuf(f"gt{i}", [64, 512], BF16, ph) for i in range(2)]),
                    Ring([sbuf(f"sg{i}", [64, 512], BF16, ph) for i in range(2)]),
                    psum[7], PS[7])

        if "C" in mixers:
          with ExitStack() as ph:
            kT = sbuf("c_kT", [128, 4, S], BF16, ph); KT = [Buf() for _ in range(4)]
            vv = sbuf("c_v", [128, 32, 512], BF16, ph); VV = [Buf() for _ in range(4)]
            qT = sbuf("c_qT", [128, 4, NOWN], BF16, ph); QT = [Buf() for _ in range(4)]
            gT = sbuf("c_gT", [128, 4, NOWN], BF16, ph); GTb = [Buf() for _ in range(4)]
            cmask = sbuf("cmask", [128, 8, 512], BF16, ph); CM = Buf()
            K.dma(K.q_sp, cmask[:], cmask_d.rearrange("p (a b) -> p a b", a=8), writes=[CM])
            for c in range(4):
                K.dma(K.q_sp, kT[:, c, :], C_kT[c * 128:(c + 1) * 128, :], writes=[KT[c]])
                K.dma(K.q_sp, qT[:, c, :], C_qT[c * 128:(c + 1) * 128, :], writes=[QT[c]])
                K.dma(K.q_sp, gT[:, c, :], C_gT[c * 128:(c + 1) * 128, :], writes=[GTb[c]])
                K.dma(K.q_sp, vv[:, c * 8:(c + 1) * 8, :],
                      C_v[c * 1024:(c + 1) * 1024, :].rearrange("(kb p) c -> p kb c", p=128), writes=[VV[c]])
            er = Ring([sbuf(f"ce{i}", [128, 512], F32, ph) for i in range(2)])
            spr = Ring([sbuf(f"csp{i}", [128, 512], F32, ph) for i in range(3)])
            Lr = Ring([sbuf(f"cL{i}", [128, 512], F32, ph) for i in range(3)])
            argr = Ring([sbuf(f"carg{i}", [128, 512], F32, ph) for i in range(2)])
            ar = Ring([sbuf(f"ca{i}", [128, 512], BF16, ph) for i in range(3)])
            raccr = Ring([sbuf(f"cr{i}", [128, 512], F32, ph) for i in range(2)])
            sgr = Ring([sbuf(f"csg{i}", [128, 512], BF16, ph) for i in range(2)])
            zi = 0
            for j in range(4):
                tok = slice(j * 512, (j + 1) * 512)
                nkb = 8 * (j + 1)
                for h in range(8):
                    hp, hh = h // 2, h % 2
                    prt = slice(64 * hh, 64 * hh + 64)
                    yps_t, YPB = psum[4 + (h % 2)], PS[4 + (h % 2)]
                    racc = None
                    for kb in range(nkb - 1, -1, -1):
                        diag = kb >= 8 * j
                        zp, ZP = psum[zi % 2], PS[zi % 2]
                        ap_, AP_ = psum[2 + zi % 2], PS[2 + zi % 2]
                        zi += 1
                        K.op(K.pe, lambda e: e.matmul(zp[:, :], lhsT=kT[prt, hp, kb * 128:(kb + 1) * 128], rhs=qT[prt, hp, tok],
                                                      start=True, stop=True), [KT[hp], QT[hp]], [ZP])
                        et, EB = er.next()
                        K.op(K.act, lambda e: e.activation(out=et[:], in_=zp[:, :], func=AF.Exp, scale=-scale), [ZP], [EB])
                        spt, SPB = spr.next()
                        K.op(K.act, lambda e: e.activation(out=spt[:], in_=et[:], func=AF.Ln, bias=1.0, scale=1.0), [EB], [SPB])
                        Lt, LB = Lr.next()
                        K.op(K.dve, lambda e: e.scalar_tensor_tensor(out=Lt[:], in0=zp[:, :], scalar=scale, in1=spt[:],
                                                                     op0=ALU.mult, op1=ALU.add), [ZP, SPB], [LB])
                        if diag:
                            K.op(K.pool, lambda e: e.tensor_tensor(out=Lt[:], in0=Lt[:], in1=cmask[:, kb - 8 * j, :], op=ALU.mult),
                                 [LB, CM], [LB])
                        first = racc is None
                        K.op(K.pe, lambda e: e.matmul(ap_[:, :], lhsT=triu[:, :], rhs=Lt[:], start=True, stop=first),
                             [TRIU, LB], [AP_])
                        if not first:
                            rt, RB = racc
                            K.op(K.pe, lambda e: e.matmul(ap_[:, :], lhsT=onesM[:, :], rhs=rt[:], start=False, stop=True),
                                 [ONES, RB], [AP_])
                        nrt, NRB = raccr.next()
                        if first:
                            K.op(K.pool, lambda e: e.tensor_copy(out=nrt[:], in_=Lt[:]), [LB], [NRB])
                        else:
                            K.op(K.pool, lambda e: e.tensor_tensor(out=nrt[:], in0=rt[:], in1=Lt[:], op=ALU.add), [RB, LB], [NRB])
                        racc = (nrt, NRB)
                        at_, AB = argr.next()
                        K.op(K.dve, lambda e: e.tensor_tensor(out=at_[:], in0=ap_[:, :], in1=spt[:], op=ALU.add), [AP_, SPB], [AB])
                        aa, AA = ar.next()
                        K.op(K.act, lambda e: e.activation(out=aa[:], in_=at_[:], func=AF.Exp, scale=-1.0), [AB], [AA])
                        if diag:
                            K.op(K.pool, lambda e: e.tensor_tensor(out=aa[:], in0=aa[:], in1=cmask[:, kb - 8 * j, :], op=ALU.mult),
                                 [AA, CM], [AA])
                        K.op(K.pe, lambda e: e.matmul(yps_t[:, :], lhsT=vv[:, kb, hp * 128:(hp + 1) * 128], rhs=aa[:],
                                                      start=(kb == nkb - 1), stop=(kb == 0)), [VV[kb // 8], AA], [YPB])
                    sg, SG = sgr.next()
                    K.op(K.dve, lambda e: e.tensor_tensor(out=sg[prt, :], in0=yps_t[prt, :], in1=gT[prt, hp, tok], op=ALU.mult),
                         [YPB, GTb[hp]], [SG])
                    K.dma(K.q_sp, Y_T[1024 + h * 64:1024 + (h + 1) * 64, tok], sg[prt, :], reads=[SG])
          K.barrier()

        if "B" in mixers:
          with ExitStack() as ph:
            kT = sbuf("b_kT", [128, 2, S], BF16, ph); KT = Buf()
            va = sbuf("b_va", [128, 32, 130], BF16, ph); VA = Buf()
            qT = sbuf("b_qT", [128, 4, NOWN], BF16, ph); QT = [Buf() for _ in range(4)]
            bmask = sbuf("bmask", [128, 3, 128], BF16, ph); BM = Buf()
            K.dma(K.q_sp, bmask[:], bmask_d.rearrange("p (a b) -> p a b", a=3), writes=[BM])
            for kvh in range(2):
                for half in range(2):
                    K.dma(K.q_sp, kT[64 * half:64 * half + 64, kvh, :], B_kT[kvh * 64:(kvh + 1) * 64, :], writes=[KT])
            K.dma(K.q_sp, va[:], B_va.rearrange("(kb p) c -> p kb c", p=128), writes=[VA])
            for c in range(4):
                K.dma(K.q_sp, qT[:, c, :], B_qT[c * 128:(c + 1) * 128, :], writes=[QT[c]])
            er = Ring([sbuf(f"be{i}", [128, 3, 128], BF16, ph) for i in range(3)])
            pmr = Ring([sbuf(f"bpm{i}", [128, 3, 128], BF16, ph) for i in range(3)])
            rings = fin_rings(ph)
            zi = 0
            for j in range(4):
                for h in range(8):
                    hp, hh, kvh = h // 2, h % 2, h // 4
                    prt = slice(64 * hh, 64 * hh + 64)
                    yps_t, YPB = psum[4 + (h % 2)], PS[4 + (h % 2)]
                    for m in range(4):
                        i = 4 * j + m
                        idxs = [ix for ix in range(3) if 2 * i - 1 + ix >= 0]
                        zp, ZP = psum[zi % 3], PS[zi % 3]
                        zi += 1
                        for ix in idxs:
                            kb = 2 * i - 1 + ix
                            K.op(K.pe, lambda e: e.matmul(zp[:, ix * 128:(ix + 1) * 128], lhsT=kT[prt, kvh, kb * 128:(kb + 1) * 128],
                                                          rhs=qT[prt, hp, i * 128:(i + 1) * 128], start=True, stop=True),
                                 [KT, QT[hp]], [ZP])
                        i0 = idxs[0]
                        et, EB = er.next()
                        K.op(K.act, lambda e: e.activation(out=et[:, i0:3, :], in_=zp[:, i0 * 128:384].rearrange("p (a b) -> p a b", b=128),
                                                           func=AF.Exp, scale=scale), [ZP], [EB])
                        pm, PM = pmr.next()
                        K.op(K.pool, lambda e: e.tensor_tensor(out=pm[:, i0:3, :], in0=et[:, i0:3, :], in1=bmask[:, i0:3, :], op=ALU.mult),
                             [EB, BM], [PM])
                        for ix in idxs:
                            kb = 2 * i - 1 + ix
                            K.op(K.pe, lambda e: e.matmul(yps_t[0:65, m * 128:(m + 1) * 128], lhsT=va[:, kb, kvh * 65:(kvh + 1) * 65],
                                                          rhs=pm[:, ix, :], start=(ix == idxs[0]), stop=(ix == 2)), [VA, PM], [YPB])
                    finalize(yps_t, YPB, h, j, B_gT, 512, esink[64:65, h:h + 1], rings)
          K.barrier()

        if "A" in mixers:
          with ExitStack() as ph:
            ikT = sbuf("ikT", [128, S], BF16, ph); IK = Buf()
            iqT = sbuf("iqT", [128, 2, NOWN], BF16, ph); IQ = Buf()
            iw = sbuf("iw", [128, 16, 4], F32, ph); IW = Buf()
            qmask = sbuf("qmask", [128, 4, 1024], BF16, ph); QM = Buf()
            bisw = sbuf("bisw", [128, NBIS], F32, ph); BW = Buf()
            for half in range(2):
                K.dma(K.q_sp, ikT[64 * half:64 * half + 64, :], I_kT[:, :], writes=[IK])
            K.dma(K.q_sp, iqT[:], I_qT.rearrange("(c p) t -> p c t", p=128), writes=[IQ])
            K.dma(K.q_sp, iw[:], I_w.rearrange("(i p) h -> p i h", p=128), writes=[IW])
            K.dma(K.q_sp, qmask[:], qmask_d.rearrange("p (a b) -> p a b", a=4), writes=[QM])
            K.dma(K.q_sp, bisw[:], bisw_d[:, :], writes=[BW])
            scs = [sbuf(f"sc{i}", [128, S], F32, ph) for i in range(4)]; SCB = [Buf() for _ in range(4)]
            junk = sbuf("junk", [128, S], BF16, ph); JK = Buf()
            mq = Ring([sbuf(f"mq{i}", [128, S], BF16, ph) for i in range(2)])
            rr = Ring([sbuf(f"rl{i}", [128, 512], F32, ph) for i in range(3)])
            stt = Ring([sbuf(f"bst{i}", [128, 4, 8], F32, ph) for i in range(2)])
            Wt = Ring([sbuf(f"bW{i}", [128, 4, NBIS], F32, ph) for i in range(2)])
            mst = Ring([sbuf(f"mst{i}", [128, 8, 128], BF16, ph) for i in range(3)])
            zi = 0
            for j in range(4):
                nk = 1024 * (j + 1)
                st_, STB = stt.next()
                W_, WB_ = Wt.next()
                for m in range(4):
                    i = 4 * j + m
                    sc = scs[m]
                    for kc in range(nk // 512):
                        for hI in range(4):
                            prt = slice(64 * (hI % 2), 64 * (hI % 2) + 64)
                            zp, ZP = psum[zi % 3], PS[zi % 3]
                            zi += 1
                            K.op(K.pe, lambda e: e.matmul(zp[:, :], lhsT=iqT[prt, hI // 2, i * 128:(i + 1) * 128],
                                                          rhs=ikT[prt, kc * 512:(kc + 1) * 512], start=True, stop=True), [IQ, IK], [ZP])
                            rt, RB = rr.next()
                            K.op(K.act, lambda e: e.activation(out=rt[:], in_=zp[:, :], func=AF.Relu, scale=0.0625), [ZP], [RB])
                            dst = sc[:, kc * 512:(kc + 1) * 512]
                            if hI == 0:
                                K.op(K.dve, lambda e: e.tensor_scalar(out=dst, in0=rt[:], scalar1=iw[:, i, 0:1], scalar2=None, op0=ALU.mult),
                                     [RB, IW], [SCB[m]])
                            else:
                                K.op(K.dve, lambda e: e.scalar_tensor_tensor(out=dst, in0=rt[:], scalar=iw[:, i, hI:hI + 1], in1=dst,
                                                                             op0=ALU.mult, op1=ALU.add), [RB, IW, SCB[m]], [SCB[m]])
                    K.op(K.dve, lambda e: e.tensor_reduce(out=st_[:, m, 0:1], in_=sc[:, 0:nk], axis=AX.X, op=ALU.max,
                                                          apply_absolute_value=True), [SCB[m]], [STB])
                    K.op(K.dve, lambda e: e.tensor_tensor(out=sc[:, nk - 1024:nk], in0=sc[:, nk - 1024:nk], in1=qmask[:, m, :], op=ALU.add),
                         [SCB[m], QM], [SCB[m]])
                    K.op(K.dve, lambda e: e.tensor_scalar(out=st_[:, m, 1:2], in0=st_[:, m, 0:1], scalar1=-1.001, scalar2=None, op0=ALU.mult),
                         [STB], [STB])
                    K.op(K.dve, lambda e: e.tensor_scalar(out=W_[:, m, :], in0=bisw[:, :], scalar1=st_[:, m, 0:1], scalar2=None, op0=ALU.mult),
                         [BW, STB], [WB_])
                for it in range(NBIS):
                    K.op(K.dve, lambda e: e.tensor_tensor(out=st_[:, :, 2:3], in0=st_[:, :, 1:2], in1=W_[:, :, it:it + 1], op=ALU.add),
                         [STB, WB_], [STB])
                    for m in range(4):
                        K.op(K.dve, lambda e: e.tensor_scalar(out=junk[:, 0:nk], in0=scs[m][:, 0:nk], scalar1=st_[:, m, 2:3], scalar2=0.0,
                                                              op0=ALU.is_ge, op1=ALU.add, accum_out=st_[:, m, 3:4]),
                             [SCB[m], STB], [JK, STB])
                    K.op(K.dve, lambda e: e.scalar_tensor_tensor(out=st_[:, :, 4:5], in0=st_[:, :, 3:4], scalar=255.5, in1=W_[:, :, it:it + 1],
                                                                 op0=ALU.is_ge, op1=ALU.mult), [STB, WB_], [STB])
                    K.op(K.dve, lambda e: e.tensor_tensor(out=st_[:, :, 1:2], in0=st_[:, :, 1:2], in1=st_[:, :, 4:5], op=ALU.add),
                         [STB], [STB])
                for m in range(4):
                    i = 4 * j + m
                    mqt, MQB = mq.next()
                    K.op(K.pool, lambda e: e.tensor_scalar(out=mqt[:, 0:nk], in0=scs[m][:, 0:nk], scalar1=st_[:, m, 1:2], scalar2=None,
                                                           op0=ALU.is_ge), [SCB[m], STB], [MQB])
                    for g in range(nk // 1024):
                        zp, ZP = psum[4 + zi % 2], PS[4 + zi % 2]
                        zi += 1
                        zb = zp[:].bitcast(BF16)
                        for t in range(8):
                            kb = g * 8 + t
                            K.op(K.pe, lambda e: e.transpose(zb[:, t * 128:(t + 1) * 128], mqt[:, kb * 128:(kb + 1) * 128], ident[:]),
                                 [MQB, IDENT], [ZP])
                        ms, MS = mst.next()
                        K.op(K.act, lambda e: e.copy(out=ms[:], in_=zb.rearrange("p (a b) -> p a b", a=8)), [ZP], [MS])
                        K.dma(K.q_sp, MaskT[g * 1024:(g + 1) * 1024, i * 128:(i + 1) * 128].rearrange("(kb p) q -> p kb q", p=128),
                              ms[:], reads=[MS])
          K.barrier()
          with ExitStack() as ph:
            kT = sbuf("a_kT", [128, 4, S], BF16, ph); KT = [Buf() for _ in range(4)]
            va = sbuf("a_va", [128, 32, 520], BF16, ph); VA = [Buf() for _ in range(4)]
            qT = sbuf("a_qT", [128, 4, NOWN], BF16, ph); QT = [Buf() for _ in range(4)]
            for c in range(4):
                K.dma(K.q_sp, kT[:, c, :], A_kT[c * 128:(c + 1) * 128, :], writes=[KT[c]])
                K.dma(K.q_sp, qT[:, c, :], A_qT[c * 128:(c + 1) * 128, :], writes=[QT[c]])
                K.dma(K.q_sp, va[:, c * 8:(c + 1) * 8, :],
                      A_va[c * 1024:(c + 1) * 1024, :].rearrange("(kb p) c -> p kb c", p=128), writes=[VA[c]])
            mTr = Ring([sbuf(f"mT{i}", [128, 32, 512], BF16, ph) for i in range(2)])
            er = Ring([sbuf(f"ae{i}", [128, 512], BF16, ph) for i in range(3)])
            pmr = Ring([sbuf(f"apm{i}", [128, 512], BF16, ph) for i in range(3)])
            rings = fin_rings(ph)
            zi = 0
            for j in range(4):
                tok = slice(j * 512, (j + 1) * 512)
                nkb = 8 * (j + 1)
                mT, MTB = mTr.next()
                K.dma(K.q_sp, mT[:, 0:nkb, :], MaskT[0:nkb * 128, tok].rearrange("(kb p) q -> p kb q", p=128), writes=[MTB])
                for h in range(8):
                    hp, hh = h // 2, h % 2
                    prt = slice(64 * hh, 64 * hh + 64)
                    yps_t, YPB = psum[4 + (h % 2)], PS[4 + (h % 2)]
                    for kb in range(nkb):
                        zp, ZP = psum[zi % 3], PS[zi % 3]
                        zi += 1
                        K.op(K.pe, lambda e: e.matmul(zp[:, :], lhsT=kT[prt, hp, kb * 128:(kb + 1) * 128], rhs=qT[prt, hp, tok],
                                                      start=True, stop=True), [KT[hp], QT[hp]], [ZP])
                        et, EB = er.next()
                        K.op(K.act, lambda e: e.activation(out=et[:], in_=zp[:, :], func=AF.Exp, scale=scale), [ZP], [EB])
                        pm, PM = pmr.next()
                        K.op(K.pool, lambda e: e.tensor_tensor(out=pm[:], in0=et[:], in1=mT[:, kb, :], op=ALU.mult), [EB, MTB], [PM])
                        K.op(K.pe, lambda e: e.matmul(yps_t[0:65, :], lhsT=va[:, kb, h * 65:(h + 1) * 65], rhs=pm[:],
                                                      start=(kb == 0), stop=(kb == nkb - 1)), [VA[kb // 8], PM], [YPB])
                    finalize(yps_t, YPB, h, j, A_gT, 0, None, rings)
          K.barrier()

        if stop_after == "D":
            return nc
        with ExitStack() as ph:
            wst = Ring([sbuf(f"wst{i}", [128, 4096], F32, ph) for i in range(2)])
            wbr = sbuf("wbr", [128, 12, D], BF16, ph); WBR = [Buf() for _ in range(3)]
            wo = sbuf("wo", [128, 8, D], BF16, ph); WO = [Buf() for _ in range(2)]
            for b in range(3):
                wt, WF = wst.next()
                K.dma(K.q_sp, wt[:].rearrange("p (c n) -> p c n", c=4), w_br[b].rearrange("(c p) n -> p c n", p=128), writes=[WF])
                K.op(K.pool, lambda e: e.tensor_copy(out=wbr[:, b * 4:(b + 1) * 4, :], in_=wt[:].rearrange("p (c n) -> p c n", c=4)),
                     [WF], [WBR[b]])
            for nh in range(2):
                wt, WF = wst.next()
                K.dma(K.q_sp, wt[:].rearrange("p (c n) -> p c n", c=8),
                      w_out[:, nh * 512:(nh + 1) * 512].rearrange("(c p) n -> p c n", p=128), writes=[WF])
                K.op(K.pool, lambda e: e.tensor_copy(out=wo[:, :, nh * 512:(nh + 1) * 512], in_=wt[:].rearrange("p (c n) -> p c n", c=8)),
                     [WF], [WO[nh]])
            yTr = Ring([sbuf(f"yT{i}", [128, 12, 512], BF16, ph) for i in range(2)])
            sgr = Ring([sbuf(f"esg{i}", [128, 512], BF16, ph) for i in range(4)])
            mar = Ring([sbuf(f"ema{i}", [128, 512], F32, ph) for i in range(3)])
            ttr = Ring([sbuf(f"ett{i}", [128, 512], F32, ph) for i in range(3)])
            mgr = Ring([sbuf(f"emg{i}", [128, 8, 512], BF16, ph) for i in range(2)])
            xor_ = Ring([sbuf(f"exo{i}", [128, D], F32, ph) for i in range(2)])
            xnr = Ring([sbuf(f"exn{i}", [128, D], F32, ph) for i in range(2)])
            o2r = Ring([sbuf(f"eo2{i}", [128, D], F32, ph) for i in range(2)])
            ssr = Ring([sbuf(f"ess{i}", [128, 4], F32, ph) for i in range(2)])
            zi = 0
            for j in range(4):
                tok = slice(j * 512, (j + 1) * 512)
                yT, YTB = yTr.next()
                K.dma(K.q_sp, yT[:], Y_T[:, tok].rearrange("(c p) t -> p c t", p=128), writes=[YTB])
                mg, MG = mgr.next()
                for nn in range(8):
                    macc = None
                    for b in range(3):
                        zp, ZP = psum[zi % 3], PS[zi % 3]
                        zi += 1
                        for c in range(4):
                            K.op(K.pe, lambda e: e.matmul(zp[:, :], lhsT=wbr[:, b * 4 + c, nn * 128:(nn + 1) * 128], rhs=yT[:, b * 4 + c, :],
                                                          start=(c == 0), stop=(c == 3)), [WBR[b], YTB], [ZP])
                        sg, SG = sgr.next()
                        K.dma(K.q_act, sg[:], M_T[b * D + nn * 128:b * D + (nn + 1) * 128, tok], writes=[SG])
                        if b == 0:
                            ma_, MA = mar.next()
                            K.op(K.dve, lambda e: e.tensor_tensor(out=ma_[:], in0=zp[:, :], in1=sg[:], op=ALU.mult), [ZP, SG], [MA])
                            macc = (ma_, MA)
                        else:
                            tt, TT = ttr.next()
                            K.op(K.dve, lambda e: e.tensor_tensor(out=tt[:], in0=zp[:, :], in1=sg[:], op=ALU.mult), [ZP, SG], [TT])
                            pa, PA = macc
                            if b == 1:
                                ma_, MA = mar.next()
                                K.op(K.pool, lambda e: e.tensor_tensor(out=ma_[:], in0=pa[:], in1=tt[:], op=ALU.add), [PA, TT], [MA])
                                macc = (ma_, MA)
                            else:
                                K.op(K.pool, lambda e: e.tensor_tensor(out=mg[:, nn, :], in0=pa[:], in1=tt[:], op=ALU.add), [PA, TT], [MG])
                for sub in range(4):
                    rows = slice(j * 512 + sub * 128, j * 512 + (sub + 1) * 128)
                    xo, XO = xor_.next()
                    K.dma(K.q_act, xo[:], x_own[rows, :], writes=[XO])
                    xn, XN = xnr.next()
                    for nh in range(2):
                        zp, ZP = psum[4 + zi % 2], PS[4 + zi % 2]
                        zi += 1
                        for c in range(8):
                            K.op(K.pe, lambda e: e.matmul(zp[:, :], lhsT=mg[:, c, sub * 128:(sub + 1) * 128], rhs=wo[:, c, nh * 512:(nh + 1) * 512],
                                                          start=(c == 0), stop=(c == 7)), [MG, WO[nh]], [ZP])
                        tt, TT = ttr.next()
                        K.op(K.dve, lambda e: e.tensor_tensor(out=tt[:], in0=zp[:, :], in1=gt_bc[:, nh * 512:(nh + 1) * 512], op=ALU.mult),
                             [ZP, GTBC], [TT])
                        K.op(K.pool, lambda e: e.tensor_tensor(out=xn[:, nh * 512:(nh + 1) * 512], in0=tt[:], in1=xo[:, nh * 512:(nh + 1) * 512],
                                                               op=ALU.add), [TT, XO], [XN])
                    if final_norm:
                        o2, O2 = o2r.next()
                        ss, SSB = ssr.next()
                        K.op(K.act, lambda e: e.activation(out=o2[:], in_=xn[:], func=AF.Square, accum_out=ss[:, 0:1]), [XN], [O2, SSB])
                        K.op(K.act, lambda e: e.activation(out=ss[:, 1:2], in_=ss[:, 0:1], func=AF.Sqrt, scale=1.0 / D, bias=EPS), [SSB], [SSB])
                        K.op(K.dve, lambda e: e.reciprocal(out=ss[:, 2:3], in_=ss[:, 1:2]), [SSB], [SSB])
                        K.op(K.dve, lambda e: e.scalar_tensor_tensor(out=o2[:], in0=xn[:], scalar=ss[:, 2:3], in1=fg_bc[:],
                                                                     op0=ALU.mult, op1=ALU.mult), [XN, SSB, FGBC, O2], [O2])
                        K.dma(K.q_sp, y_out[rows, :], o2[:], reads=[O2])
                    else:
                        K.dma(K.q_sp, y_out[rows, :], xn[:], reads=[XN])
        K.barrier()
    return nc


def _bf(a):
    return np.ascontiguousarray(a.astype(ml_dtypes.bfloat16))


def _const_tables():
    inv = (1.0 / (10000.0 ** (np.arange(0, 64, 2, dtype=np.float32) / np.float32(64)))).astype(np.float32)
    ang = (np.arange(S, dtype=np.float32)[:, None] * inv[None, :]).astype(np.float32)
    cos = np.cos(ang).astype(np.float32)
    sin = np.sin(ang).astype(np.float32)
    i = np.arange(128)
    cosT = np.ascontiguousarray(cos[:, i % 32].T)
    sgn = np.where((i % 64) < 32, -1.0, 1.0).astype(np.float32)
    sinT = np.ascontiguousarray((sin[:, i % 32] * sgn[None, :]).T)
    return cosT, sinT


def _core_consts(p, cosT, sinT):
    own_tok = (np.arange(16)[:, None] * 256 + p * 128 + np.arange(128)[None, :]).reshape(-1)
    s = np.arange(128)[:, None]
    kb = np.arange(8)[None, :, None]
    q = np.arange(512)[None, None, :]
    qpos = (2 * (q // 128) + p) * 128 + (q % 128)
    kpos = kb * 128 + s[:, :, None]
    cmask = (kpos < qpos).astype(np.float32).reshape(128, 8 * 512)
    t = np.arange(128)[:, None, None]
    m = np.arange(4)[None, :, None]
    kp = np.arange(1024)[None, None, :]
    qp = (2 * m + p) * 128 + t
    qmask = np.where(kp <= qp, 0.0, NEG).astype(np.float32).reshape(128, 4 * 1024)
    idx = np.arange(3)[None, :, None]
    tq = np.arange(128)[None, None, :]
    kpos_b = (idx - 1) * 128 + s[:, :, None]
    qpos_b = p * 128 + tq
    bmask = ((kpos_b <= qpos_b) & (kpos_b > qpos_b - 128)).astype(np.float32).reshape(128, 3 * 128)
    return dict(cos_own=np.ascontiguousarray(cosT[:, own_tok]), sin_own=np.ascontiguousarray(sinT[:, own_tok]),
                cmask=_bf(cmask), qmask=_bf(qmask), bmask=_bf(bmask)), own_tok


def _rot_cols():
    cols = []
    for fam, w in ROT_FAMS:
        j = np.arange(w)
        cols.append(OFF[fam] + (j // 64) * 64 + ((j % 64) + 32) % 64)
    return np.concatenate(cols)


def make_in_maps(xcur, inputs, layer):
    cosT, sinT = _const_tables()
    rot = _rot_cols()
    w_in = np.ascontiguousarray(inputs["w_in"][layer])
    shared = dict(
        w_ada=np.ascontiguousarray(inputs["w_ada"][layer]),
        b_ada=np.ascontiguousarray(inputs["b_ada"][layer][None, :]),
        norm_g=np.ascontiguousarray(inputs["norm_g"][layer][None, :]),
        w_in=w_in,
        w_rot=np.ascontiguousarray(w_in[:, rot]),
        sinks=np.ascontiguousarray(inputs["sinks"][layer][None, :]),
        w_br=np.ascontiguousarray(np.stack([inputs["w_br_a"][layer], inputs["w_br_b"][layer], inputs["w_br_c"][layer]])),
        w_out=np.ascontiguousarray(inputs["w_out"][layer]),
        final_g=np.ascontiguousarray(inputs["final_g"][None, :]),
        cos_all=cosT, sin_all=sinT,
        ident=_bf(np.eye(128, dtype=np.float32)),
        triu=np.ascontiguousarray((np.arange(128)[:, None] > np.arange(128)[None, :]).astype(np.float32)),
        bisw=np.ascontiguousarray(np.broadcast_to((2.002 * 0.5 ** (np.arange(NBIS) + 1)).astype(np.float32)[None, :], (128, NBIS))),
    )
    maps = []
    owns = []
    for core in range(8):
        b, p = core // 2, core % 2
        cc, own_tok = _core_consts(p, cosT, sinT)
        m = dict(shared)
        m.update(cc)
        m["x_all"] = np.ascontiguousarray(xcur[b])
        m["x_own"] = np.ascontiguousarray(xcur[b][own_tok])
        m["c_col"] = np.ascontiguousarray(inputs["c"][b].reshape(8, 128).T)
        maps.append(m)
        owns.append(own_tok)
    return maps, owns


_PROGS = {}


def _get_prog(final_norm):
    if final_norm not in _PROGS:
        _PROGS[final_norm] = build_program(final_norm)
    return _PROGS[final_norm]


def kernel(**inputs):
    inputs = {k: np.asarray(v) for k, v in inputs.items()}
    xcur = np.ascontiguousarray(inputs["x"].astype(np.float32))
    nlayers = inputs["w_in"].shape[0]
    for layer in range(nlayers):
        final = layer == nlayers - 1
        nc = _get_prog(final)
        maps, owns = make_in_maps(xcur, inputs, layer)
        res = run_bass_kernel_spmd(nc, maps, core_ids=list(range(8)))
        xn = np.empty_like(xcur)
        for core in range(8):
            xn[core // 2][owns[core]] = res.results[core]["y_own"]
        xcur = xn
    return xcur
```
